# Optimizing a Trainium2 kernel written in Bass

```python
import jax, jax.numpy as jnp
from jax import lax
import numpy as np

D_MODEL = 1024
BATCH = 8
SEQ = 2048
DEPTH = 4

N_MIXERS = 3
HEAD_DIM = 64
RMS_EPS = 1e-6

NSA_HEADS = 16
NSA_GROUPS = 4
NSA_REP = NSA_HEADS // NSA_GROUPS
NSA_KV = NSA_GROUPS * HEAD_DIM
CMP_BLOCK = 32
CMP_STRIDE = 16
CMP_HIDDEN = 256
SLC_BLOCK = 64
SLC_TOPK = 16
WINDOW = 512
WIN_QBLOCK = 128
NSA_Q_CHUNK = 64
NSA_IN = NSA_HEADS * HEAD_DIM + 6 * NSA_KV + 3 * NSA_HEADS
NSA_SPLITS = [NSA_HEADS * HEAD_DIM + i * NSA_KV for i in range(7)]

MOBA_HEADS = 16
MOBA_BLOCK = 256
MOBA_TOPK = 3
MOBA_Q_CHUNK = 32

CONV_WIDTH = 3

D_FF = -(-8 * D_MODEL // (3 * 256)) * 256

kernel_name = "hybrid_nsa_moba_shortconv_trunk"


def rms_norm(x, g):
    xf = x.astype(jnp.float32)
    y = xf * lax.rsqrt(jnp.mean(xf * xf, axis=-1, keepdims=True) + RMS_EPS)
    return (y * g.astype(jnp.float32)).astype(x.dtype)


def masked_softmax(s, mask):
    s = jnp.where(mask, s.astype(jnp.float32), -jnp.inf)
    m = jnp.max(s, axis=-1, keepdims=True)
    m = jnp.where(jnp.isfinite(m), m, 0.0)
    e = jnp.where(mask, jnp.exp(s - m), 0.0)
    d = jnp.sum(e, axis=-1, keepdims=True)
    return e / jnp.where(d > 0, d, 1.0)


def gather_blocks(blocks, idx):
    return jax.vmap(jax.vmap(lambda b, i: b[i]))(blocks, idx)


def swiglu(a, w_gate, w_up, w_down):
    return (jax.nn.silu(a @ w_gate) * (a @ w_up)) @ w_down


def nsa_mixer(h, w_in, w_out, pos_k, pos_v, k_w1, k_w2, v_w1, v_w2):
    B_, S_, _ = h.shape
    G, R, dh = NSA_GROUPS, NSA_REP, HEAD_DIM
    q, k_c, v_c, k_s, v_s, k_w, v_w, gates = jnp.split(h @ w_in, NSA_SPLITS, axis=-1)
    q = q.reshape(B_, S_, G, R, dh) * (HEAD_DIM ** -0.5)
    kvs = [a.reshape(B_, S_, G, dh) for a in (k_c, v_c, k_s, v_s, k_w, v_w)]
    k_c, v_c, k_s, v_s, k_w, v_w = kvs
    gates = jax.nn.sigmoid(gates.astype(jnp.float32)).reshape(B_, S_, 3, G, R)
    t = jnp.arange(S_)

    n_cmp = (S_ - CMP_BLOCK) // CMP_STRIDE + 1
    cmp_idx = jnp.arange(n_cmp)[:, None] * CMP_STRIDE + jnp.arange(CMP_BLOCK)[None, :]

    def compress(kv, pos, w1, w2):
        blk = kv[:, cmp_idx] + pos[None, None, :, None, :]
        blk = jnp.moveaxis(blk, 3, 2).reshape(B_, n_cmp, G, CMP_BLOCK * dh)
        return jax.nn.gelu(blk @ w1) @ w2

    kc = compress(k_c, pos_k, k_w1, k_w2)
    vc = compress(v_c, pos_v, v_w1, v_w2)
    cmp_end = jnp.arange(n_cmp) * CMP_STRIDE + CMP_BLOCK - 1
    cmp_mask = cmp_end[None, :] <= t[:, None]
    s_cmp = jnp.einsum('bsgrd,bcgd->bgrsc', q, kc)
    p_cmp = masked_softmax(s_cmp, cmp_mask)
    o_cmp = jnp.einsum('bgrsc,bcgd->bsgrd', p_cmp.astype(vc.dtype), vc)

    n_slc = S_ // SLC_BLOCK
    c_start = jnp.arange(n_cmp) * CMP_STRIDE
    j_start = jnp.arange(n_slc) * SLC_BLOCK
    overlap = ((c_start[:, None] < j_start[None, :] + SLC_BLOCK)
               & (c_start[:, None] + CMP_BLOCK > j_start[None, :])).astype(jnp.float32)
    imp = jnp.einsum('bgrsc,cn->bgsn', p_cmp, overlap)
    cur = t // SLC_BLOCK
    blk_ids = jnp.arange(n_slc)
    forced = ((blk_ids[None, :] == 0) | (blk_ids[None, :] == cur[:, None])
              | (blk_ids[None, :] == cur[:, None] - 1))
    imp = jnp.where(forced, jnp.inf,
                    jnp.where(blk_ids[None, :] <= cur[:, None], imp, -jnp.inf))
    n_sel = min(SLC_TOPK, n_slc)
    _, sel = lax.top_k(imp, n_sel)

    kb = k_s.reshape(B_, n_slc, SLC_BLOCK, G, dh).transpose(0, 3, 1, 2, 4)
    vb = v_s.reshape(B_, n_slc, SLC_BLOCK, G, dh).transpose(0, 3, 1, 2, 4)
    Qc = NSA_Q_CHUNK
    n_qc = S_ // Qc
    q_chunks = q.reshape(B_, n_qc, Qc, G, R, dh).transpose(1, 0, 2, 3, 4, 5)
    sel_chunks = sel.reshape(B_, G, n_qc, Qc, n_sel).transpose(2, 0, 1, 3, 4)

    def slc_chunk(args):
        ci, qc, ix = args
        tq = ci * Qc + jnp.arange(Qc)
        kg = gather_blocks(kb, ix)
        vg = gather_blocks(vb, ix)
        kpos = ix[..., None] * SLC_BLOCK + jnp.arange(SLC_BLOCK)
        mask = (kpos <= tq[:, None, None]).reshape(B_, G, Qc, n_sel * SLC_BLOCK)[:, :, None]
        s = jnp.einsum('bqgrd,bgqnkd->bgrqnk', qc, kg).reshape(B_, G, R, Qc, n_sel * SLC_BLOCK)
        p = masked_softmax(s, mask).reshape(B_, G, R, Qc, n_sel, SLC_BLOCK)
        return jnp.einsum('bgrqnk,bgqnkd->bqgrd', p.astype(vg.dtype), vg)

    o_slc = lax.map(slc_chunk, (jnp.arange(n_qc), q_chunks, sel_chunks))
    o_slc = o_slc.transpose(1, 0, 2, 3, 4, 5).reshape(B_, S_, G, R, dh)

    QB = WIN_QBLOCK
    nqb = S_ // QB
    n_back = WINDOW // QB
    pad = n_back * QB

    def band(kv):
        kp = jnp.pad(kv, ((0, 0), (pad, 0), (0, 0), (0, 0))).reshape(B_, nqb + n_back, QB, G, dh)
        return jnp.concatenate([kp[:, i:i + nqb] for i in range(n_back + 1)], axis=2)

    kwb = band(k_w)
    vwb = band(v_w)
    qb = q.reshape(B_, nqb, QB, G, R, dh)
    qpos = t.reshape(nqb, QB)
    kpos = (jnp.arange(nqb)[:, None] - n_back) * QB + jnp.arange((n_back + 1) * QB)[None, :]
    wmask = ((kpos[:, None, :] <= qpos[:, :, None])
             & (kpos[:, None, :] > qpos[:, :, None] - WINDOW)
             & (kpos[:, None, :] >= 0))
    s_win = jnp.einsum('bnqgrd,bnkgd->bgrnqk', qb, kwb)
    p_win = masked_softmax(s_win, wmask)
    o_win = jnp.einsum('bgrnqk,bnkgd->bnqgrd', p_win.astype(vwb.dtype), vwb).reshape(B_, S_, G, R, dh)

    o = (gates[:, :, 0, :, :, None] * o_cmp + gates[:, :, 1, :, :, None] * o_slc
         + gates[:, :, 2, :, :, None] * o_win).astype(h.dtype)
    return o.reshape(B_, S_, NSA_HEADS * dh) @ w_out


def moba_mixer(h, w_in, w_out):
    B_, S_, _ = h.shape
    H, dh = MOBA_HEADS, HEAD_DIM
    q, k, v = jnp.split(h @ w_in, 3, axis=-1)
    n_blk = -(-S_ // MOBA_BLOCK)
    S_pad = n_blk * MOBA_BLOCK

    def heads(a):
        a = jnp.pad(a, ((0, 0), (0, S_pad - S_), (0, 0)))
        return a.reshape(B_, S_pad, H, dh).transpose(0, 2, 1, 3)

    q = heads(q) * (HEAD_DIM ** -0.5)
    k = heads(k)
    v = heads(v)
    kb = k.reshape(B_, H, n_blk, MOBA_BLOCK, dh)
    vb = v.reshape(B_, H, n_blk, MOBA_BLOCK, dh)
    k_mean = jnp.mean(kb.astype(jnp.float32), axis=3)
    t = jnp.arange(S_pad)
    cur = t // MOBA_BLOCK
    n_sel = min(MOBA_TOPK, n_blk - 1)
    if n_sel > 0:
        score = jnp.einsum('bhsd,bhnd->bhsn', q.astype(jnp.float32), k_mean)
        past = jnp.arange(n_blk)[None, :] < cur[:, None]
        _, sel = lax.top_k(jnp.where(past, score, -jnp.inf), n_sel)
    else:
        sel = jnp.zeros((B_, H, S_pad, 0), jnp.int32)
    Qc = MOBA_Q_CHUNK
    n_qc = S_pad // Qc
    q_chunks = q.reshape(B_, H, n_qc, Qc, dh).transpose(2, 0, 1, 3, 4)
    sel_chunks = sel.reshape(B_, H, n_qc, Qc, n_sel).transpose(2, 0, 1, 3, 4)

    def chunk(args):
        ci, qc, ix = args
        tq = ci * Qc + jnp.arange(Qc)
        own = (ci * Qc) // MOBA_BLOCK
        k_own = lax.dynamic_index_in_dim(kb, own, axis=2, keepdims=False)
        v_own = lax.dynamic_index_in_dim(vb, own, axis=2, keepdims=False)
        own_pos = own * MOBA_BLOCK + jnp.arange(MOBA_BLOCK)
        m_own = jnp.broadcast_to(own_pos[None, :] <= tq[:, None], (B_, H, Qc, MOBA_BLOCK))
        s_own = jnp.einsum('bhqd,bhkd->bhqk', qc, k_own)
        if n_sel == 0:
            p = masked_softmax(s_own, m_own)
            return jnp.einsum('bhqk,bhkd->bhqd', p.astype(v_own.dtype), v_own)
        kg = gather_blocks(kb, ix)
        vg = gather_blocks(vb, ix)
        n_keys = n_sel * MOBA_BLOCK
        s_sel = jnp.einsum('bhqd,bhqnkd->bhqnk', qc, kg).reshape(B_, H, Qc, n_keys)
        valid = ix < own
        m_sel = jnp.broadcast_to(valid[..., None], (B_, H, Qc, n_sel, MOBA_BLOCK)).reshape(B_, H, Qc, n_keys)
        p = masked_softmax(jnp.concatenate([s_sel, s_own], axis=-1),
                           jnp.concatenate([m_sel, m_own], axis=-1))
        p_sel = p[..., :n_keys].reshape(B_, H, Qc, n_sel, MOBA_BLOCK).astype(vg.dtype)
        p_own = p[..., n_keys:].astype(v_own.dtype)
        return (jnp.einsum('bhqnk,bhqnkd->bhqd', p_sel, vg)
                + jnp.einsum('bhqk,bhkd->bhqd', p_own, v_own))

    o = lax.map(chunk, (jnp.arange(n_qc), q_chunks, sel_chunks))
    o = o.transpose(1, 2, 0, 3, 4).reshape(B_, H, S_pad, dh)[:, :, :S_]
    o = o.transpose(0, 2, 1, 3).reshape(B_, S_, H * dh)
    return o @ w_out


def short_conv_mixer(h, w_in, conv_w, w_out):
    d = h.shape[-1]
    b_gate, c_gate, u = jnp.split(h @ w_in, 3, axis=-1)
    z = c_gate * u
    y = lax.conv_general_dilated(
        z, conv_w[:, None, :].astype(z.dtype), window_strides=(1,),
        padding=[(CONV_WIDTH - 1, 0)],
        dimension_numbers=('NWC', 'WIO', 'NWC'), feature_group_count=d)
    return (b_gate * y) @ w_out


def setup_inputs(seed: int = 0) -> dict:
    key = jax.random.key(seed)
    ks = jax.random.split(key, 20)
    f32 = jnp.float32
    n_a = (DEPTH + 2) // 3
    n_b = (DEPTH + 1) // 3
    n_c = DEPTH // 3

    def w(k, shape, fan_in):
        return jax.random.normal(k, shape, f32) * (fan_in ** -0.5)

    def gain(k, shape):
        return 1.0 + 0.02 * jax.random.normal(k, shape, f32)

    return {
        "x": jax.random.normal(ks[0], (BATCH, SEQ, D_MODEL), f32),
        "norm_mix": gain(ks[1], (DEPTH, D_MODEL)),
        "norm_ffn": gain(ks[2], (DEPTH, D_MODEL)),
        "norm_final": gain(ks[3], (D_MODEL,)),
        "ffn_w_gate": w(ks[4], (DEPTH, D_MODEL, D_FF), D_MODEL),
        "ffn_w_up": w(ks[5], (DEPTH, D_MODEL, D_FF), D_MODEL),
        "ffn_w_down": w(ks[6], (DEPTH, D_FF, D_MODEL), D_FF),
        "nsa_w_in": w(ks[7], (n_a, D_MODEL, NSA_IN), D_MODEL),
        "nsa_w_out": w(ks[8], (n_a, NSA_HEADS * HEAD_DIM, D_MODEL), NSA_HEADS * HEAD_DIM),
        "nsa_cmp_pos_k": 0.1 * jax.random.normal(ks[9], (n_a, CMP_BLOCK, HEAD_DIM), f32),
        "nsa_cmp_pos_v": 0.1 * jax.random.normal(ks[10], (n_a, CMP_BLOCK, HEAD_DIM), f32),
        "nsa_cmp_k_w1": w(ks[11], (n_a, CMP_BLOCK * HEAD_DIM, CMP_HIDDEN), CMP_BLOCK * HEAD_DIM),
        "nsa_cmp_k_w2": w(ks[12], (n_a, CMP_HIDDEN, HEAD_DIM), CMP_HIDDEN),
        "nsa_cmp_v_w1": w(ks[13], (n_a, CMP_BLOCK * HEAD_DIM, CMP_HIDDEN), CMP_BLOCK * HEAD_DIM),
        "nsa_cmp_v_w2": w(ks[14], (n_a, CMP_HIDDEN, HEAD_DIM), CMP_HIDDEN),
        "moba_w_in": w(ks[15], (n_b, D_MODEL, 3 * MOBA_HEADS * HEAD_DIM), D_MODEL),
        "moba_w_out": w(ks[16], (n_b, MOBA_HEADS * HEAD_DIM, D_MODEL), MOBA_HEADS * HEAD_DIM),
        "conv_w_in": w(ks[17], (n_c, D_MODEL, 3 * D_MODEL), D_MODEL),
        "conv_w": w(ks[18], (n_c, CONV_WIDTH, D_MODEL), CONV_WIDTH),
        "conv_w_out": w(ks[19], (n_c, D_MODEL, D_MODEL), D_MODEL),
    }


def reference(x, norm_mix, norm_ffn, norm_final, ffn_w_gate, ffn_w_up, ffn_w_down,
              nsa_w_in, nsa_w_out, nsa_cmp_pos_k, nsa_cmp_pos_v, nsa_cmp_k_w1,
              nsa_cmp_k_w2, nsa_cmp_v_w1, nsa_cmp_v_w2, moba_w_in, moba_w_out,
              conv_w_in, conv_w, conv_w_out):
    h = x
    for i in range(DEPTH):
        kind, j = i % N_MIXERS, i // N_MIXERS
        a = rms_norm(h, norm_mix[i])
        if kind == 0:
            m = nsa_mixer(a, nsa_w_in[j], nsa_w_out[j], nsa_cmp_pos_k[j], nsa_cmp_pos_v[j],
                          nsa_cmp_k_w1[j], nsa_cmp_k_w2[j], nsa_cmp_v_w1[j], nsa_cmp_v_w2[j])
        elif kind == 1:
            m = moba_mixer(a, moba_w_in[j], moba_w_out[j])
        else:
            m = short_conv_mixer(a, conv_w_in[j], conv_w[j], conv_w_out[j])
        h = h + m
        a = rms_norm(h, norm_ffn[i])
        h = h + swiglu(a, ffn_w_gate[i], ffn_w_up[i], ffn_w_down[i])
    return rms_norm(h, norm_final)
```

```python
import os
import numpy as np
import concourse.bass as bass
import concourse.mybir as mybir
from concourse.bass_utils import run_bass_kernel_spmd

F32 = mybir.dt.float32
BF16 = mybir.dt.bfloat16
AF = mybir.ActivationFunctionType
ALU = mybir.AluOpType
AX = mybir.AxisListType

D = 1024
SEQ = 2048
NCH = 8
DFF = 2816
NTC = 4
NTT = 16
EPS = 1e-6
NEG = -30000.0

NSA_IN = 2608


class DSem:
    def __init__(self, sem):
        self.sem = sem
        self.count = 0


class Buf:
    def __init__(self, name, const=False, excl=False):
        self.name = name
        self.const = const
        self.excl = excl
        self.w = None
        self.r = {}


class Op:
    __slots__ = ("eng", "fn", "deps", "dma", "seq", "block", "signal", "sigcount", "waits")

    def __init__(self, eng, fn, deps, dma, seq, block):
        self.eng = eng
        self.fn = fn
        self.deps = deps
        self.dma = dma
        self.seq = seq
        self.block = block
        self.signal = False
        self.sigcount = None
        self.waits = None


ENGINES = ("pe", "act", "dve", "pool", "sp")


class Sched:
    def __init__(self, nc, stack):
        self.nc = nc
        self.stack = stack
        self.sems = {e: stack.enter_context(nc.semaphore("sem_" + e)) for e in ENGINES}
        self.sigcounts = {e: 0 for e in ENGINES}
        self.seqs = {e: 0 for e in ENGINES}
        self.ops = {e: [] for e in ENGINES}
        self.block = 0
        self.waited = {e: {} for e in ENGINES}
        self.dwaited = {e: {} for e in ENGINES}
        self.n_dsem = 0
        self.n_instr = 0

    def dsem(self, name=None):
        self.n_dsem += 1
        return DSem(self.stack.enter_context(self.nc.semaphore(f"{name or 'dsem'}_{self.n_dsem}")))

    def op(self, eng, fn, reads=(), writes=(), dma=None):
        deps = []
        for b in reads:
            if b.w is not None:
                deps.append(("raw", b.w))
            if b.excl:
                deps.extend(("war", x) for k_, x in b.r.items() if k_ != eng)
        for b in writes:
            if b.w is not None:
                deps.append(("waw", b.w))
            deps.extend(("war", x) for x in b.r.values())
        if dma is not None:
            dma.count += 16
            dmainfo = (dma, dma.count)
        else:
            dmainfo = None
        o = Op(eng, fn, deps, dmainfo, self.seqs[eng], self.block)
        self.seqs[eng] += 1
        self.ops[eng].append(o)
        mydep = ("dma", dma, dma.count) if dma is not None else ("op", o)
        for b in writes:
            b.w = mydep
            b.r = {}
        for b in reads:
            if not b.const:
                key = ("dma", id(dma)) if dma is not None else eng
                b.r[key] = mydep
        return o

    def flush(self):
        nc = self.nc
        for e in ENGINES:
            for o in self.ops[e]:
                best = {}
                dbest = {}
                for kind, d in o.deps:
                    if d[0] == "dma":
                        ds, cnt = d[1], d[2]
                        if dbest.get(id(ds), (None, 0))[1] < cnt:
                            dbest[id(ds)] = (ds, cnt)
                    else:
                        p = d[1]
                        if p.block != self.block:
                            continue
                        if p is o:
                            continue
                        if p.eng == e and (kind == "war" or e == "pe"):
                            continue
                        if p.eng not in best or best[p.eng].seq < p.seq:
                            best[p.eng] = p
                waits = []
                for pe_, p in best.items():
                    if self.waited[e].get(pe_, -1) >= p.seq:
                        continue
                    self.waited[e][pe_] = p.seq
                    p.signal = True
                    waits.append(("op", p))
                for _, (ds, cnt) in dbest.items():
                    if self.dwaited[e].get(id(ds), 0) >= cnt:
                        continue
                    self.dwaited[e][id(ds)] = cnt
                    waits.append(("dma", ds, cnt))
                o.waits = waits
        for e in ENGINES:
            for o in self.ops[e]:
                if o.signal:
                    self.sigcounts[e] += 1
                    o.sigcount = self.sigcounts[e]
        engobj = {"pe": "tensor", "act": "scalar", "dve": "vector", "pool": "gpsimd", "sp": "sync"}
        sched = self

        def emit_all(e, eng):
            for o in sched.ops[e]:
                for w in o.waits:
                    if w[0] == "op":
                        eng.wait_ge(sched.sems[w[1].eng], w[1].sigcount)
                    else:
                        eng.wait_ge(w[1].sem, w[2])
                inst = o.fn(eng)
                sched.n_instr += 1
                if o.dma is not None:
                    inst.then_inc(o.dma[0].sem, 16)
                    assert not o.signal
                elif o.signal:
                    inst.then_inc(sched.sems[e], 1)

        with nc.Block(no_gpsimd_drain=True) as block:
            for e in ENGINES:
                if not self.ops[e]:
                    continue
                getattr(block, engobj[e])(lambda eng, e=e: emit_all(e, eng))
        for e in ENGINES:
            self.ops[e] = []
        self.block += 1


class Prog:
    def __init__(self, layers, final_norm, out_token_major=False):
        from contextlib import ExitStack
        self.layers = layers
        self.final_norm = final_norm
        self.stack = ExitStack()
        nc = self.nc = bass.Bass("TRN2", target_bir_lowering=False)
        self.S = Sched(nc, self.stack)
        self.dram = {}

    def din(self, name, shape, dtype=F32):
        t = self.nc.dram_tensor(name, list(shape), dtype, kind="ExternalInput").ap()
        self.dram[name] = t
        return t

    def un(self, name):
        self._uid = getattr(self, "_uid", 0) + 1
        return f"{name}_{self._uid}"

    def sb(self, name, shape, dtype):
        return self.stack.enter_context(self.nc.sbuf_tensor(name, list(shape), dtype))

    def ps(self, name, shape, dtype):
        return self.stack.enter_context(self.nc.psum_tensor(name, list(shape), dtype))


    def mm(self, out, lhsT, rhs, start, stop, reads, writes, skip=False):
        return self.S.op("pe", lambda e: e.matmul(out, lhsT=lhsT, rhs=rhs, start=start, stop=stop,
                                                  skip_group_check=skip), reads=reads, writes=writes)

    def tr(self, out, in_, ident, reads, writes):
        return self.S.op("pe", lambda e: e.transpose(out, in_, ident), reads=reads, writes=writes)

    def act(self, out, in_, func, reads, writes, bias=None, scale=None, accum_out=None):
        kw = {}
        if bias is not None:
            kw["bias"] = bias
        if scale is not None:
            kw["scale"] = scale
        if accum_out is not None:
            kw["accum_out"] = accum_out
        return self.S.op("act", lambda e: e.activation(out=out, in_=in_, func=func, **kw), reads=reads, writes=writes)

    def tt(self, out, in0, in1, op, reads, writes, eng="dve"):
        return self.S.op(eng, lambda e: e.tensor_tensor(out=out, in0=in0, in1=in1, op=op), reads=reads, writes=writes)

    def ts(self, out, in0, s1, s2, op0, op1, reads, writes, eng="dve"):
        if op1 is None:
            return self.S.op(eng, lambda e: e.tensor_scalar(out=out, in0=in0, scalar1=s1, scalar2=None, op0=op0),
                             reads=reads, writes=writes)
        return self.S.op(eng, lambda e: e.tensor_scalar(out=out, in0=in0, scalar1=s1, scalar2=s2, op0=op0, op1=op1),
                         reads=reads, writes=writes)

    def stt(self, out, in0, scalar, in1, op0, op1, reads, writes):
        return self.S.op("dve", lambda e: e.scalar_tensor_tensor(out=out, in0=in0, scalar=scalar, in1=in1,
                                                                 op0=op0, op1=op1), reads=reads, writes=writes)

    def cp(self, out, in_, reads, writes, eng="dve"):
        return self.S.op(eng, lambda e: e.tensor_copy(out=out, in_=in_), reads=reads, writes=writes)

    def recip(self, out, in_, reads, writes):
        return self.S.op("dve", lambda e: e.reciprocal(out=out, in_=in_), reads=reads, writes=writes)

    def vmax(self, out, in_, reads, writes):
        return self.S.op("dve", lambda e: e.max(out=out, in_=in_), reads=reads, writes=writes)

    def reduce(self, out, in_, op, reads, writes):
        return self.S.op("dve", lambda e: e.tensor_reduce(out=out, in_=in_, axis=AX.X, op=op), reads=reads, writes=writes)

    def memset(self, ap, val, writes, eng="dve"):
        return self.S.op(eng, lambda e: e.memset(ap, val), writes=writes)

    def dma(self, eng, out, in_, reads, writes, dsem):
        return self.S.op(eng, lambda e: e.dma_start(out=out, in_=in_), reads=reads, writes=writes, dma=dsem)

    def setup(self):
        nc, S = self.nc, self.S
        self.xT = self.din("xT", [D, SEQ])
        self.d_gains = self.din("gains", [128, 9 * NCH])
        self.d_wg = self.din("ffn_w_gate", [4, D, DFF])
        self.d_wu = self.din("ffn_w_up", [4, D, DFF])
        self.d_wd = self.din("ffn_w_down", [4, DFF, D])
        self.d_nsa_w_out = self.din("nsa_w_out", [2, D, D])
        self.d_k_w1 = self.din("nsa_cmp_k_w1", [2, 2048, 256])
        self.d_k_w2 = self.din("nsa_cmp_k_w2", [2, 256, 64])
        self.d_v_w1 = self.din("nsa_cmp_v_w1", [2, 2048, 256])
        self.d_v_w2 = self.din("nsa_cmp_v_w2", [2, 256, 64])
        self.d_moba_w_in = self.din("moba_w_in", [1, D, 3 * D])
        self.d_moba_w_out = self.din("moba_w_out", [1, D, D])
        self.d_conv_w_in = self.din("conv_w_in", [1, D, 3 * D])
        self.d_conv_w = self.din("conv_w_l", [128, 3 * NCH])
        self.d_conv_w_out = self.din("conv_w_out", [1, D, D])
        self.outT = nc.dram_tensor("outT", [D, SEQ], F32, kind="ExternalOutput").ap()

        self.hT = self.sb("hT", [128, NCH, SEQ], F32)
        self.aT = self.sb("aT", [128, NCH, SEQ], BF16)
        self.hbuf = [[Buf(f"h{c}_{tc}") for tc in range(NTC)] for c in range(NCH)]
        self.abuf = [Buf(f"a{tc}") for tc in range(NTC)]
        self.NSLOT = 8
        self.wring = self.sb("wring", [128, self.NSLOT, 2048], BF16)
        self.wbuf = [Buf(f"w{i}") for i in range(self.NSLOT)]
        self.wsem = [S.dsem(f"wsem{i}") for i in range(self.NSLOT)]
        self.wnext = 0
        self.gains = self.sb("gains_sb", [128, 9, NCH], F32)
        self.gbuf = Buf("gains")
        self.ones_bf = self.sb("ones_bf", [128, 128], BF16)
        self.epsb = self.sb("epsb", [128, 1], F32)
        self.cbuf = Buf("consts")
        self.csem = S.dsem("csem")
        self.cpsem = S.dsem("cpsem")
        self.xsem = S.dsem("xsem")
        self.osem = S.dsem("osem")
        self.bank = [self.ps(f"bank{i}", [128, 512], F32) for i in range(8)]
        self.bbuf = [Buf(f"bank{i}", excl=True) for i in range(8)]

        for c in range(NCH):
            self.dma("sp", self.hT[:, c, :], self.xT[c * 128:(c + 1) * 128, :], [],
                     [self.hbuf[c][tc] for tc in range(NTC)], S.dsem(f"xsem{c}"))
        self.dma("sp", self.gains[:].rearrange("p a c -> p (a c)"), self.d_gains[:, :], [], [self.gbuf], S.dsem("gsem"))
        self.memset(self.ones_bf[:], 1.0, [self.cbuf])
        self.memset(self.epsb[:], EPS, [self.cbuf])
        self.attn_consts()
        self.nsa_consts()

    def wload(self, dst_view_fn, src_ap):
        i = self.wnext
        self.wnext = (self.wnext + 1) % self.NSLOT
        dst = dst_view_fn(self.wring[:, i, :])
        self.dma("pool", dst, src_ap, [], [self.wbuf[i]], self.wsem[i])
        return i, self.wbuf[i], dst

    def rmsnorm(self, gidx, sq_t, rs_t, sqb, rsb, out_fn, out_bufs, banks=(6, 7)):
        for tc in range(NTC):
            tsl = slice(tc * 512, (tc + 1) * 512)
            bk = banks[tc % 2]
            for c in range(NCH):
                k = c % 2
                self.act(sq_t[:, k, :], self.hT[:, c, tsl], AF.Square, [self.hbuf[c][tc]], [sqb[k]])
                self.mm(self.bank[bk][:], self.ones_bf[:], sq_t[:, k, :], c == 0, c == NCH - 1,
                        [sqb[k], self.cbuf], [self.bbuf[bk]])
            k2 = tc % 2
            self.act(rs_t[:, k2, :], self.bank[bk][:], AF.Sqrt, [self.bbuf[bk], self.cbuf], [rsb[k2]],
                     bias=self.epsb[:, 0:1], scale=1.0 / D)
            self.recip(rs_t[:, k2, :], rs_t[:, k2, :], [rsb[k2]], [rsb[k2]])
            for c in range(NCH):
                self.stt(out_fn(c, tc), self.hT[:, c, tsl], self.gains[:, gidx, c:c + 1], rs_t[:, k2, :],
                         ALU.mult, ALU.mult, [self.hbuf[c][tc], rsb[k2], self.gbuf], out_bufs(c, tc))

    def norm_to_a(self, gidx, sq_t, rs_t, sqb, rsb):
        self.rmsnorm(gidx, sq_t, rs_t, sqb, rsb,
                     lambda c, tc: self.aT[:, c, tc * 512:(tc + 1) * 512],
                     lambda c, tc: [self.abuf[tc]])

    def ffn_phase(self, li):
        from contextlib import ExitStack
        S, nc = self.S, self.nc
        with ExitStack() as ph:
            sq_t = ph.enter_context(nc.sbuf_tensor(self.un("f_sq"), [128, 2, 512], BF16))
            rs_t = ph.enter_context(nc.sbuf_tensor(self.un("f_rs"), [128, 2, 512], F32))
            sg_t = ph.enter_context(nc.sbuf_tensor(self.un("f_sg"), [128, 2, 512], F32))
            hid_t = ph.enter_context(nc.sbuf_tensor(self.un("f_hid"), [128, 2, 2, 512], BF16))
            sqb = [Buf("sq0"), Buf("sq1")]
            rsb = [Buf("rs0"), Buf("rs1")]
            sgb = [Buf("sg0"), Buf("sg1")]
            hidb = [[Buf(f"hid{i}{j}") for j in range(2)] for i in range(2)]
            self.norm_to_a(4 + li, sq_t, rs_t, sqb, rsb)
            NU = DFF // 256
            items = [(u, tc) for u in range(NU) for tc in range(NTC)]
            wcur = {}

            def load_unit(u):
                cs = slice(u * 256, (u + 1) * 256)
                g = self.wload(lambda v: v.rearrange("p (c n) -> p c n", c=NCH),
                               self.d_wg[li, :, cs].rearrange("(c p) n -> p c n", p=128))
                up = self.wload(lambda v: v.rearrange("p (c n) -> p c n", c=NCH),
                                self.d_wu[li, :, cs].rearrange("(c p) n -> p c n", p=128))
                dn = self.wload(lambda v: v.rearrange("p (j n) -> p j n", j=2),
                                self.d_wd[li, cs, :].rearrange("(j p) n -> p j n", p=128))
                wcur[u] = (g, up, dn)

            def stage_a(idx):
                u, tc = items[idx]
                if tc == 0:
                    if u == 0:
                        load_unit(0)
                    if u + 1 < NU:
                        load_unit(u + 1)
                (gi, gb, gv), (ui, ub, uv), _ = wcur[u]
                tsl = slice(tc * 512, (tc + 1) * 512)
                par = idx % 2
                for jj in range(2):
                    bg, bu = jj, 2 + jj
                    for c in range(NCH):
                        self.mm(self.bank[bg][:], gv[:, c, jj * 128:(jj + 1) * 128], self.aT[:, c, tsl],
                                c == 0, c == NCH - 1, [gb, self.abuf[tc]], [self.bbuf[bg]])
                    for c in range(NCH):
                        self.mm(self.bank[bu][:], uv[:, c, jj * 128:(jj + 1) * 128], self.aT[:, c, tsl],
                                c == 0, c == NCH - 1, [ub, self.abuf[tc]], [self.bbuf[bu]])
                    self.act(sg_t[:, jj, :], self.bank[bg][:], AF.Silu, [self.bbuf[bg]], [sgb[jj]])
                    self.tt(hid_t[:, par, jj, :], sg_t[:, jj, :], self.bank[bu][:], ALU.mult,
                            [sgb[jj], self.bbuf[bu]], [hidb[par][jj]])

            def stage_b(idx):
                u, tc = items[idx]
                _, _, (di, db, dv) = wcur[u]
                tsl = slice(tc * 512, (tc + 1) * 512)
                par = idx % 2
                for c in range(NCH):
                    bk = 4 + (c % 2)
                    for jj in range(2):
                        self.mm(self.bank[bk][:], dv[:, jj, c * 128:(c + 1) * 128], hid_t[:, par, jj, :],
                                jj == 0, jj == 1, [db, hidb[par][jj]], [self.bbuf[bk]])
                    self.tt(self.hT[:, c, tsl], self.hT[:, c, tsl], self.bank[bk][:], ALU.add,
                            [self.hbuf[c][tc], self.bbuf[bk]], [self.hbuf[c][tc]])

            for idx in range(len(items) + 1):
                if idx < len(items):
                    stage_a(idx)
                if idx >= 1:
                    stage_b(idx - 1)
            S.flush()

    def final_phase(self):
        from contextlib import ExitStack
        S, nc = self.S, self.nc
        with ExitStack() as ph:
            sq_t = ph.enter_context(nc.sbuf_tensor(self.un("n_sq"), [128, 2, 512], BF16))
            rs_t = ph.enter_context(nc.sbuf_tensor(self.un("n_rs"), [128, 2, 512], F32))
            sqb = [Buf("sq0"), Buf("sq1")]
            rsb = [Buf("rs0"), Buf("rs1")]
            if self.final_norm:
                ot = ph.enter_context(nc.sbuf_tensor(self.un("n_out"), [128, NCH, SEQ], F32))
                obufs = [[Buf(f"o{c}_{tc}") for tc in range(NTC)] for c in range(NCH)]
                self.rmsnorm(8, sq_t, rs_t, sqb, rsb,
                             lambda c, tc: ot[:, c, tc * 512:(tc + 1) * 512],
                             lambda c, tc: [obufs[c][tc]])
            else:
                ot, obufs = self.hT, self.hbuf
            for c in range(NCH):
                self.dma("sp", self.outT[c * 128:(c + 1) * 128, :], ot[:, c, :],
                         [obufs[c][tc] for tc in range(NTC)], [], self.osem)
            done = Buf("done")
            done.w = ("dma", self.osem, self.osem.count)
            S.op("sp", lambda e: e.nop(), reads=[done])
            S.flush()

    def build(self):
        self.setup()
        self.S.flush()
        for li in self.layers:
            kind = li % 3
            if kind == 0:
                self.nsa_phase(li)
            elif kind == 1:
                self.moba_phase(li)
            elif kind == 2:
                self.conv_phase(li)
            if not os.environ.get('NO_FFN'):
                self.ffn_phase(li)
        self.final_phase()
        return self.nc

    def nsa_consts(self):
        self.d_wq = self.din("nsa_wq", [2, 4, D, 256])
        self.d_wkv = self.din("nsa_wkv", [2, 4, D, 512])
        self.d_wgt = self.din("nsa_wg", [2, D, 48])
        self.d_posT = self.din("nsa_posT", [2, 2, 128, 16])
        self.d_ind32 = self.din("c_ind32", [32, SEQ])
        self.d_ovl = self.din("c_ovl", [127, 32])
        self.d_tri2 = self.din("c_tri2", [128, 128])
        self.d_selp = self.din("c_selp", [128, 2 * 62])
        self.d_ebig = self.din("c_ebig", [9, 247])
        self.d_bm = self.din("c_bm", [9, 128])
        self.tri2 = self.sb("tri2", [128, 128], BF16)
        self.selp = self.sb("selp", [128, 2, 62], F32)
        self.ebig = self.sb("ebig", [9, 247], BF16)
        self.bm = self.sb("bm", [9, 128], BF16)
        self.tri2b, self.selpb, self.ebigb, self.bmb = Buf("tri2"), Buf("selp"), Buf("ebig"), Buf("bm")
        self.dma("pool", self.tri2[:], self.d_tri2[:, :], [], [self.tri2b], self.S.dsem("tri2sem"))
        self.dma("sp", self.selp[:].rearrange("p a b -> p (a b)"), self.d_selp[:, :], [], [self.selpb], self.S.dsem("selpsem"))
        self.dma("pool", self.ebig[:], self.d_ebig[:, :], [], [self.ebigb], self.S.dsem("ebigsem"))
        self.dma("pool", self.bm[:], self.d_bm[:, :], [], [self.bmb], self.S.dsem("bmsem"))

    def nsa_phase(self, li):
        from contextlib import ExitStack
        import os
        S, nc = self.S, self.nc
        j = li // 3
        bank, bbuf = self.bank, self.bbuf
        NG = int(os.environ.get("NSA_G", "4"))
        NQT = int(os.environ.get("NSA_QT", "16"))
        with ExitStack() as ph:
            def sbt(name, shape, dt):
                return ph.enter_context(nc.sbuf_tensor(self.un(name), shape, dt))
            sq_t = sbt("n_sq", [128, 2, 512], BF16)
            rs_t = sbt("n_rs", [128, 2, 512], F32)
            sqb = [Buf("sq0"), Buf("sq1")]
            rsb = [Buf("rs0"), Buf("rs1")]
            W2 = sbt("n_W2", [128, 2, 2, 64], BF16)
            posT = sbt("n_posT", [128, 2, 16], BF16)
            cb = sbt("n_cb", [128, 2, 2], F32)
            WG = sbt("n_WG", [128, NCH, 48], BF16)
            GT = sbt("n_GT", [128, NTT, 48], F32)
            C2 = sbt("n_C2", [128, SEQ], BF16)
            KSA = sbt("n_KSA", [96, SEQ], BF16)
            KW = sbt("n_KW", [64, SEQ], BF16)
            VSW = sbt("n_VSW", [128, NTT, 2, 65], BF16)
            KCT = sbt("n_KCT", [64, 128], BF16)
            VCA = sbt("n_VCA", [128, 97], BF16)
            gx = sbt("n_gx", [128, 4, 128], F32)
            hidT = sbt("n_hidT", [128, 2, 128], BF16)
            QA = sbt("n_QA", [96, 2, 4, 512], BF16)
            pt = sbt("n_pt", [128, 3, 512], BF16)
            pcm = sbt("n_pcm", [128, 2, 512], BF16)
            stg = sbt("n_stg", [128, 2, 96], BF16)
            rden = sbt("n_rden", [128, 2, 4], F32)
            imt = sbt("n_imt", [128, 4, 32], F32)
            imp = sbt("n_imp", [128, 32], F32)
            impm = sbt("n_impm", [128, 32], F32)
            imw = sbt("n_imw", [128, 32], F32)
            mx8 = sbt("n_mx8", [128, 2, 8], F32)
            thr = sbt("n_thr", [128, 1], F32)
            coef = sbt("n_coef", [128, 3, 4], F32)
            oacc = sbt("n_oacc", [128, 3, 4, 64], F32)
            otmp = sbt("n_otmp", [128, 4, 64], F32)
            otok = sbt("n_otok", [128, 2, 4, 256], BF16)
            oT = sbt("n_oT", [128, 2, 2, 512], BF16)
            B = Buf
            W2b, posTb, cbb, WGb, GTb = B("W2"), B("posT"), B("cb"), B("WG"), B("GT")
            C2b, KSAb, KWb, VSWb, KCTb, VCAb = B("C2"), B("KSA"), B("KW"), B("VSW"), B("KCT"), B("VCA")
            gxb, hidTb = B("gx"), B("hidT")
            qab = [[B(f"qa{p}{q}") for q in range(4)] for p in range(2)]
            ptb = [B(f"pt{i}") for i in range(3)]
            pcmb = [B("pcm0"), B("pcm1")]
            stgb = [B("stg0"), B("stg1")]
            rdenb, imtb, impb, impmb, imwb, mx8b, thrb, coefb = (B("rden"), B("imt"), B("imp"), B("impm"), B("imw"),
                                                                 B("mx8"), B("thr"), B("coef"))
            oaccb = [B("oacc0"), B("oacc1"), B("oacc2")]
            otmpb = B("otmp")
            otb = [B("ot0"), B("ot1")]
            oTb = [B("oT0"), B("oT1")]
            self.dma("pool", W2[:, 0, :, :], self.d_k_w2[j].rearrange("(h p) n -> p h n", p=128), [], [W2b], S.dsem("w2ksem"))
            self.dma("pool", W2[:, 1, :, :], self.d_v_w2[j].rearrange("(h p) n -> p h n", p=128), [], [W2b], S.dsem("w2vsem"))
            self.dma("pool", posT[:, 0, :], self.d_posT[j, 0], [], [posTb], S.dsem("poskS"))
            self.dma("pool", posT[:, 1, :], self.d_posT[j, 1], [], [posTb], S.dsem("posvS"))
            self.dma("pool", WG[:], self.d_wgt[j].rearrange("(c p) n -> p c n", p=128), [], [WGb], S.dsem("wgsem"))
            self.dma("pool", KSA[64:96, :], self.d_ind32[:, :], [], [KSAb], S.dsem("ind32sem"))
            self.dma("pool", VCA[0:127, 65:97], self.d_ovl[:, :], [], [VCAb], S.dsem("ovlsem"))
            self.memset(VCA[:, 64:65], 1.0, [VCAb])
            self.memset(VSW[:, :, :, 64:65], 1.0, [VSWb])
            self.memset(stg[:], 0.0, stgb)
            self.norm_to_a(li, sq_t, rs_t, sqb, rsb)
            for tt in range(NTT):
                bk = 6 + (tt % 2)
                for c in range(NCH):
                    self.mm(bank[bk][:, 0:48], self.aT[:, c, tt * 128:(tt + 1) * 128], WG[:, c, :], c == 0, c == NCH - 1,
                            [WGb, self.abuf[tt // 4]], [bbuf[bk]])
                self.act(GT[:, tt, :], bank[bk][:, 0:48], AF.Sigmoid, [bbuf[bk]], [GTb])
            nmisc = [0]

            def misc_bank():
                nmisc[0] += 1
                return 6 + (nmisc[0] % 2)

            v8 = lambda v: v.rearrange("p (c n) -> p c n", c=NCH)
            for g in range(NG):
                _, wqb, wq = self.wload(v8, self.d_wq[j, g].rearrange("(c p) n -> p c n", p=128))
                _, wab, wa = self.wload(v8, self.d_wkv[j, g, :, 0:256].rearrange("(c p) n -> p c n", p=128))
                _, wbb, wb = self.wload(v8, self.d_wkv[j, g, :, 256:512].rearrange("(c p) n -> p c n", p=128))
                _, wob, wo = self.wload(lambda v: v.rearrange("p (j n) -> p j n", j=2),
                                        self.d_nsa_w_out[j, g * 256:(g + 1) * 256, :].rearrange("(j p) n -> p j n", p=128))
                w1 = []
                for kv, dsrc in enumerate([self.d_k_w1, self.d_v_w1]):
                    for half in range(2):
                        w1.append(self.wload(v8, dsrc[j, half * 1024:(half + 1) * 1024, :].rearrange("(c p) n -> p c n", p=128)))

                for kv in range(2):
                    for tc in range(NTC):
                        tsl = slice(tc * 512, (tc + 1) * 512)
                        bk = misc_bank()
                        for c in range(NCH):
                            self.mm(bank[bk][:], wa[:, c, kv * 128:(kv + 1) * 128], self.aT[:, c, tsl], c == 0, c == NCH - 1,
                                    [wab, self.abuf[tc]], [bbuf[bk]])
                        self.act(C2[0:64, tsl], bank[bk][0:64, :], AF.Copy, [bbuf[bk]], [C2b])
                        if tc == 0:
                            self.cp(C2[64:128, 0:511], bank[bk][64:128, 1:512], [bbuf[bk]], [C2b])
                        else:
                            self.cp(C2[64:128, tc * 512 - 1:tc * 512 + 511], bank[bk][64:128, :], [bbuf[bk]], [C2b])
                    bk = misc_bank()
                    for hc in range(2):
                        for jj in range(16):
                            _, w1b, w1v = w1[kv * 2 + jj // 8]
                            self.mm(bank[bk][:, hc:hc + 1], w1v[:, jj % 8, hc * 128:(hc + 1) * 128], posT[:, kv, jj:jj + 1],
                                    jj == 0, jj == 15, [w1b, posTb], [bbuf[bk]])
                    self.cp(cb[:, kv, :], bank[bk][:, 0:2], [bbuf[bk]], [cbb])
                    for hc in range(2):
                        bk = misc_bank()
                        for jj in range(16):
                            _, w1b, w1v = w1[kv * 2 + jj // 8]
                            self.mm(bank[bk][:, 0:127], w1v[:, jj % 8, hc * 128:(hc + 1) * 128],
                                    C2[:, 2 * jj:2 * jj + 2017:16], jj == 0, jj == 15, [w1b, C2b], [bbuf[bk]])
                        x0, x1, x2, x3 = (gx[:, i, 0:127] for i in range(4))
                        self.act(x0, bank[bk][:, 0:127], AF.Identity, [bbuf[bk], cbb], [gxb], bias=cb[:, kv, hc:hc + 1])
                        self.tt(x1, x0, x0, ALU.mult, [gxb], [gxb])
                        self.ts(x1, x1, 0.044715, 1.0, ALU.mult, ALU.add, [gxb], [gxb])
                        self.tt(x2, x1, x0, ALU.mult, [gxb], [gxb])
                        self.act(x3, x2, AF.Sigmoid, [gxb], [gxb], scale=1.5957691216057308)
                        self.tt(hidT[:, hc, 0:127], x0, x3, ALU.mult, [gxb], [hidTb])
                    bk = misc_bank()
                    if kv == 0:
                        for hc in range(2):
                            self.mm(bank[bk][0:64, 0:127], W2[:, 0, hc, :], hidT[:, hc, 0:127], hc == 0, hc == 1,
                                    [W2b, hidTb], [bbuf[bk]])
                        self.act(KCT[:, 0:127], bank[bk][0:64, 0:127], AF.Copy, [bbuf[bk]], [KCTb])
                    else:
                        for hc in range(2):
                            self.mm(bank[bk][0:127, 0:64], hidT[:, hc, 0:127], W2[:, 1, hc, :], hc == 0, hc == 1,
                                    [W2b, hidTb], [bbuf[bk]])
                        self.act(VCA[0:127, 0:64], bank[bk][0:127, 0:64], AF.Copy, [bbuf[bk]], [VCAb])
                for which, (dst, dstb) in enumerate([(KSA, KSAb), (KW, KWb)]):
                    for tc in range(NTC):
                        tsl = slice(tc * 512, (tc + 1) * 512)
                        bk = misc_bank()
                        for c in range(NCH):
                            self.mm(bank[bk][0:64, :], wb[:, c, which * 64:(which + 1) * 64], self.aT[:, c, tsl],
                                    c == 0, c == NCH - 1, [wbb, self.abuf[tc]], [bbuf[bk]])
                        self.act(dst[0:64, tsl], bank[bk][0:64, :], AF.Copy, [bbuf[bk]], [dstb])
                for tt in range(NTT):
                    bk = misc_bank()
                    for c in range(NCH):
                        self.mm(bank[bk][:, 0:128], self.aT[:, c, tt * 128:(tt + 1) * 128], wb[:, c, 128:256],
                                c == 0, c == NCH - 1, [wbb, self.abuf[tt // 4]], [bbuf[bk]])
                    self.cp(VSW[:, tt, :, 0:64], bank[bk][:, 0:128].rearrange("p (a d) -> p a d", a=2), [bbuf[bk]], [VSWb])

                def qproj(qc):
                    par = qc % 2
                    tsl = slice(qc * 512, (qc + 1) * 512)
                    for r in range(4):
                        bk = misc_bank()
                        for c in range(NCH):
                            self.mm(bank[bk][0:64, :], wq[:, c, r * 64:(r + 1) * 64], self.aT[:, c, tsl],
                                    c == 0, c == NCH - 1, [wqb, self.abuf[qc]], [bbuf[bk]])
                        self.act(QA[0:64, par, r, :], bank[bk][0:64, :], AF.Copy, [bbuf[bk]], qab[par], scale=0.125)

                def cmp_stage(qt):
                    qc, qi = qt // 4, qt % 4
                    par, p2 = qc % 2, qt % 2
                    qsl = slice(qi * 128, (qi + 1) * 128)
                    self.mm(bank[5][0:127, :], KCT[:, 0:127], QA[0:64, par, :, qsl], True, False,
                            [KCTb, qab[par][qi]], [bbuf[5]])
                    e0 = 120 - 8 * qt
                    self.mm(bank[5][0:127, :], self.ebig[0:9, e0:e0 + 127],
                            self.bm[0:9, :].unsqueeze(1).to_broadcast([9, 4, 128]), False, True,
                            [self.ebigb, self.bmb], [bbuf[5]])
                    self.act(pcm[0:127, p2, :], bank[5][0:127, :], AF.Exp, [bbuf[5]], [pcmb[p2]])
                    ov = bank[5][:, 0:388].rearrange("p (a b) -> p a b", a=4)
                    for r in range(4):
                        self.mm(ov[:, r, :], pcm[0:127, p2, r * 128:(r + 1) * 128], VCA[0:127, :], True, True,
                                [pcmb[p2], VCAb], [bbuf[5]])
                    self.ts(rden[:, 0, :], ov[:, :, 64], 1e-30, None, ALU.max, None, [bbuf[5]], [rdenb])
                    self.recip(rden[:, 0, :], rden[:, 0, :], [rdenb], [rdenb])
                    self.tt(imt[:], ov[:, :, 65:97], rden[:, 0, :].unsqueeze(2).to_broadcast([128, 4, 32]), ALU.mult,
                            [bbuf[5], rdenb], [imtb])
                    self.reduce(imp[:], imt[:].rearrange("p r n -> p n r"), ALU.add, [imtb], [impb])
                    m0 = 30 - 2 * qt
                    self.tt(impm[:], imp[:], self.selp[:, 0, m0:m0 + 32], ALU.mult, [impb, self.selpb], [impmb])
                    self.tt(impm[:], impm[:], self.selp[:, 1, m0:m0 + 32], ALU.add, [impmb, self.selpb], [impmb])
                    self.memset(impm[:, 0:1], 1e30, [impmb])
                    self.vmax(mx8[:, 0, :], impm[:], [impmb], [mx8b])
                    S.op("dve", lambda e: e.match_replace(out=imw[:], in_to_replace=mx8[:, 0, :], in_values=impm[:],
                                                          imm_value=-3e38), reads=[impmb, mx8b], writes=[imwb])
                    self.vmax(mx8[:, 1, :], imw[:], [imwb], [mx8b])
                    self.ts(thr[:], mx8[:, 1, 7:8], -1e29, None, ALU.max, None, [mx8b], [thrb])
                    self.ts(stg[:, p2, 64:96], impm[:], thr[:, 0:1], NEG, ALU.is_lt, ALU.mult, [impmb, thrb], [stgb[p2]])
                    self.tt(coef[:, 0, :], rden[:, 0, :], GT[:, qt, g * 4:g * 4 + 4], ALU.mult, [rdenb, GTb], [coefb])
                    p3 = qt % 3
                    self.tt(oacc[:, p3, :, :], ov[:, :, 0:64], coef[:, 0, :].unsqueeze(2).to_broadcast([128, 4, 64]), ALU.mult,
                            [bbuf[5], coefb], [oaccb[p3]])

                def bias_stage(qt):
                    qc, qi = qt // 4, qt % 4
                    par, p2 = qc % 2, qt % 2
                    bk = misc_bank()
                    bv = bank[bk][:].bitcast(BF16)
                    self.tr(bv[0:96, 0:128], stg[:, p2, :], self.ident[:], [stgb[p2], self.identb], [bbuf[bk]])
                    self.cp(QA[64:96, par, :, qi * 128:(qi + 1) * 128],
                            bv[64:96, 0:128].unsqueeze(1).to_broadcast([32, 4, 128]), [bbuf[bk]], [qab[par][qi]])

                tiles = []
                for qt in range(NQT):
                    for kt in range(max(0, qt - 4), qt + 1):
                        tiles.append(("win", qt, kt))
                    for kt in range(qt + 1):
                        tiles.append(("slc", qt, kt))
                nt = len(tiles)

                def s_stage(i):
                    kind, qt, kt = tiles[i]
                    qc, qi = qt // 4, qt % 4
                    par = qc % 2
                    sl = i % 3
                    qsl = slice(qi * 128, (qi + 1) * 128)
                    ksl = slice(kt * 128, (kt + 1) * 128)
                    if kind == "win":
                        self.mm(bank[sl][:], KW[0:64, ksl], QA[0:64, par, :, qsl], True, True,
                                [KWb, qab[par][qi]], [bbuf[sl]])
                    else:
                        self.mm(bank[sl][:], KSA[0:96, ksl], QA[0:96, par, :, qsl], True, True,
                                [KSAb, qab[par][qi]], [bbuf[sl]])
                    self.act(pt[:, sl, :], bank[sl][:], AF.Exp, [bbuf[sl]], [ptb[sl]])
                    mask = None
                    if kt == qt:
                        mask, mb = self.tri, self.trib
                    elif kind == "win" and kt == qt - 4:
                        mask, mb = self.tri2, self.tri2b
                    if mask is not None:
                        pv_ = pt[:, sl, :].rearrange("p (r q) -> p r q", r=4)
                        self.tt(pv_, pv_, mask[:].unsqueeze(1).to_broadcast([128, 4, 128]), ALU.mult,
                                [ptb[sl], mb], [ptb[sl]], eng="pool")

                def pv_stage(i):
                    kind, qt, kt = tiles[i]
                    sl = i % 3
                    p2 = qt % 3
                    if kind == "win":
                        ob, vi, first = 4, 1, (kt == max(0, qt - 4))
                    else:
                        ob, vi, first = 3, 0, (kt == 0)
                    ov = bank[ob][:, 0:260].rearrange("p (a b) -> p a b", a=4)
                    for r in range(4):
                        self.mm(ov[:, r, :], pt[:, sl, r * 128:(r + 1) * 128], VSW[:, kt, vi, :],
                                first and r == 0, kt == qt, [ptb[sl], VSWb], [bbuf[ob]], skip=True)
                    if kt == qt:
                        bi = 2 if kind == "win" else 1
                        self.ts(rden[:, 1, :], ov[:, :, 64], 1e-30, None, ALU.max, None, [bbuf[ob]], [rdenb])
                        self.recip(rden[:, 1, :], rden[:, 1, :], [rdenb], [rdenb])
                        self.tt(coef[:, bi, :], rden[:, 1, :], GT[:, qt, bi * 16 + g * 4:bi * 16 + g * 4 + 4], ALU.mult,
                                [rdenb, GTb], [coefb])
                        self.tt(otmp[:], ov[:, :, 0:64], coef[:, bi, :].unsqueeze(2).to_broadcast([128, 4, 64]), ALU.mult,
                                [bbuf[ob], coefb], [otmpb])
                        if kind == "win":
                            self.tt(oacc[:, p2, :, :], oacc[:, p2, :, :], otmp[:], ALU.add, [oaccb[p2], otmpb], [oaccb[p2]])
                        else:
                            qc, qi = qt // 4, qt % 4
                            par = qc % 2
                            self.tt(otok[:, par, qi, :].rearrange("p (r d) -> p r d", r=4), oacc[:, p2, :, :], otmp[:], ALU.add,
                                    [oaccb[p2], otmpb], [otb[par]])
                            if qi == 3 or qt == NQT - 1:
                                out_stage(qc)

                def out_stage(qc):
                    par = qc % 2
                    tsl = slice(qc * 512, (qc + 1) * 512)
                    bk = misc_bank()
                    tv = bank[bk][:].bitcast(BF16).rearrange("p (f q) -> p f q", f=2)
                    for fc in range(2):
                        for qi in range(4):
                            self.tr(tv[:, fc, qi * 128:(qi + 1) * 128], otok[:, par, qi, fc * 128:(fc + 1) * 128],
                                    self.ident[:], [otb[par], self.identb], [bbuf[bk]])
                    self.cp(oT[:, par, :, :], tv, [bbuf[bk]], [oTb[par]])
                    for c2 in range(NCH):
                        bk = misc_bank()
                        for fc in range(2):
                            self.mm(bank[bk][:], wo[:, fc, c2 * 128:(c2 + 1) * 128], oT[:, par, fc, :],
                                    fc == 0, fc == 1, [wob, oTb[par]], [bbuf[bk]])
                        self.tt(self.hT[:, c2, tsl], self.hT[:, c2, tsl], bank[bk][:], ALU.add,
                                [self.hbuf[c2][qc], bbuf[bk]], [self.hbuf[c2][qc]])

                first_tile = {}
                for i, (kind, qt, kt) in enumerate(tiles):
                    first_tile.setdefault(qt, i)
                first_slc = {}
                for i, (kind, qt, kt) in enumerate(tiles):
                    if kind == "slc":
                        first_slc.setdefault(qt, i)
                qproj(0)
                cmp_stage(0)
                bias_stage(0)
                for i in range(nt + 1):
                    if i < nt:
                        kind, qt, kt = tiles[i]
                        if first_tile[qt] == i and qt + 1 < NQT:
                            if (qt + 1) % 4 == 0:
                                qproj((qt + 1) // 4)
                            cmp_stage(qt + 1)
                        if first_slc[qt] == i and qt + 1 < NQT:
                            bias_stage(qt + 1)
                        s_stage(i)
                    if i >= 1:
                        pv_stage(i - 1)
            S.flush()

    def host_consts(self):
        k = np.arange(128)
        ident = np.eye(128, dtype=np.float32)
        tri = (k[:, None] <= k[None, :]).astype(np.float32)
        n = np.arange(8)
        past = np.where(n[None, :] < n[:, None], 0.0, -1e30).astype(np.float32)
        negown = np.where(n[None, :] != n[:, None], NEG, 0.0).astype(np.float32)
        moba = np.concatenate([past.reshape(-1), negown.reshape(-1)])[None, :].repeat(128, axis=0)
        ind8 = (np.arange(SEQ)[None, :] // 256 == n[:, None]).astype(np.float32)
        tri2 = (k[:, None] > k[None, :]).astype(np.float32)
        n32 = np.arange(32)
        ind32 = (np.arange(SEQ)[None, :] // 64 == n32[:, None]).astype(np.float32)
        cst = np.arange(127)[:, None] * 16
        jst = n32[None, :] * 64
        ovl = ((cst < jst + 64) & (cst + 32 > jst)).astype(np.float32)
        ql = np.arange(128)[:, None]
        rel = np.arange(62)[None, :] - 30
        cur = (ql >= 64).astype(np.int64)
        forced = (rel == cur) | (rel == cur - 1)
        valid = rel <= cur
        pm = (valid & ~forced).astype(np.float32)
        pa = np.where(forced, 1e30, np.where(valid, 0.0, -1e30)).astype(np.float32)
        selp = np.concatenate([pm, pa], axis=1)
        ebig = np.zeros((9, 247), np.float32)
        for jj in range(8):
            ebig[jj, jj + 119] = 1.0
        ebig[8, 127:] = 1.0
        bmm = np.zeros((9, 128), np.float32)
        for jj in range(8):
            bmm[jj] = np.where(np.arange(128) >= 16 * jj + 15, 0.0, NEG)
        bmm[8] = NEG
        return {"c_ident": ident, "c_tri": tri, "c_moba": np.ascontiguousarray(moba), "c_ind8": ind8,
                "c_tri2": tri2, "c_ind32": ind32, "c_ovl": ovl, "c_selp": np.ascontiguousarray(selp),
                "c_ebig": ebig, "c_bm": bmm}

    def attn_consts(self):
        self.d_ident = self.din("c_ident", [128, 128])
        self.d_tri = self.din("c_tri", [128, 128])
        self.d_moba = self.din("c_moba", [128, 128])
        self.d_ind8 = self.din("c_ind8", [8, SEQ])
        self.ident = self.sb("ident", [128, 128], BF16)
        self.tri = self.sb("tri", [128, 128], BF16)
        self.mobac = self.sb("mobac", [128, 2, 8, 8], F32)
        self.identb, self.trib, self.mobacb = Buf("ident"), Buf("tri"), Buf("mobac")
        self.dma("pool", self.ident[:], self.d_ident[:, :], [], [self.identb], self.S.dsem("identsem"))
        self.dma("pool", self.tri[:], self.d_tri[:, :], [], [self.trib], self.S.dsem("trisem"))
        self.dma("sp", self.mobac[:].rearrange("p a b c -> p (a b c)"), self.d_moba[:, :], [], [self.mobacb],
                 self.S.dsem("mobacsem"))

    def moba_phase(self, li):
        from contextlib import ExitStack
        S, nc = self.S, self.nc
        j = li // 3
        w_in, w_out = self.d_moba_w_in, self.d_moba_w_out
        with ExitStack() as ph:
            def sbt(name, shape, dt):
                return ph.enter_context(nc.sbuf_tensor(self.un(name), shape, dt))
            sq_t = sbt("m_sq", [128, 2, 512], BF16)
            rs_t = sbt("m_rs", [128, 2, 512], F32)
            sqb = [Buf("sq0"), Buf("sq1")]
            rsb = [Buf("rs0"), Buf("rs1")]
            KA = sbt("m_KA", [72, 4, SEQ], BF16)
            VA = sbt("m_VA", [128, NTT, 4, 65], BF16)
            QA = sbt("m_QA", [72, 2, 4, 512], BF16)
            kms = sbt("m_kms", [64, 4, 8], F32)
            kmT = sbt("m_kmT", [64, 4, 8], BF16)
            msk = sbt("m_msk", [128, 4, 8], F32)
            mx8 = sbt("m_mx8", [128, 4, 8], F32)
            thr = sbt("m_thr", [128, 4], F32)
            cmpt = sbt("m_cmp", [128, 4, 8], F32)
            bst = sbt("m_bst", [128, 4, 4, 72], BF16)
            pt = sbt("m_pt", [128, 3, 512], BF16)
            rc = sbt("m_rc", [128, 2, 4], F32)
            otok = sbt("m_otok", [128, 2, 4, 256], BF16)
            oT = sbt("m_oT", [128, 2, 2, 512], BF16)
            kab = [Buf(f"ka{r}") for r in range(4)]
            vab = Buf("va")
            qab = [[Buf(f"qa{p}{r}") for r in range(4)] for p in range(2)]
            kmsb, kmTb = Buf("kms"), Buf("kmT")
            mskb, mx8b, thrb, cmpb, bstb = Buf("msk"), Buf("mx8"), Buf("thr"), Buf("cmp"), Buf("bst")
            ptb = [Buf(f"pt{i}") for i in range(3)]
            rcb = [Buf("rc0"), Buf("rc1")]
            otb = [Buf("ot0"), Buf("ot1")]
            oTb = [Buf("oT0"), Buf("oT1")]
            bank, bbuf = self.bank, self.bbuf
            import os
            stop = int(os.environ.get("MOBA_STOP", "99"))
            isem = S.dsem("ind8sem")
            if not os.environ.get("NO_IND"):
                for r in range(4):
                    self.dma("pool", KA[64:72, r, :], self.d_ind8[:, :], [], [kab[r]], isem)
                for r in range(4):
                    kab[r].w = ("dma", isem, isem.count)
            self.memset(VA[:, :, :, 64:65], 1.0, [vab])
            self.memset(bst[:], 0.0, [bstb])
            self.norm_to_a(li, sq_t, rs_t, sqb, rsb)
            nmisc = [0]

            def misc_bank():
                nmisc[0] += 1
                return 5 + (nmisc[0] % 2)

            for g in range(int(os.environ.get('MOBA_G', '4'))):
                cs = slice(g * 256, (g + 1) * 256)
                _, wqb, wq = self.wload(lambda v: v.rearrange("p (c n) -> p c n", c=NCH),
                                        w_in[j, :, cs].rearrange("(c p) n -> p c n", p=128))
                _, wkb, wk = self.wload(lambda v: v.rearrange("p (c n) -> p c n", c=NCH),
                                        w_in[j, :, slice(D + g * 256, D + (g + 1) * 256)].rearrange("(c p) n -> p c n", p=128))
                _, wvb, wv = self.wload(lambda v: v.rearrange("p (c n) -> p c n", c=NCH),
                                        w_in[j, :, slice(2 * D + g * 256, 2 * D + (g + 1) * 256)].rearrange("(c p) n -> p c n", p=128))
                _, wob, wo = self.wload(lambda v: v.rearrange("p (j n) -> p j n", j=2),
                                        w_out[j, cs, :].rearrange("(j p) n -> p j n", p=128))
                for r in range(4):
                    for tc in range(NTC):
                        tsl = slice(tc * 512, (tc + 1) * 512)
                        bk = misc_bank()
                        for c in range(NCH):
                            self.mm(bank[bk][0:64, :], wk[:, c, r * 64:(r + 1) * 64], self.aT[:, c, tsl],
                                    c == 0, c == NCH - 1, [wkb, self.abuf[tc]], [bbuf[bk]])
                        self.act(KA[0:64, r, tsl], bank[bk][0:64, :], AF.Copy, [bbuf[bk]], [kab[r]])
                        self.reduce(kms[:, r, 2 * tc:2 * tc + 2], bank[bk][0:64, :].rearrange("p (a b) -> p a b", a=2),
                                    ALU.add, [bbuf[bk]], [kmsb])
                self.ts(kmT[:], kms[:], 1.0 / 256.0, None, ALU.mult, None, [kmsb], [kmTb])
                if stop <= 1:
                    break
                for tt in range(NTT):
                    bk = misc_bank()
                    for c in range(NCH):
                        self.mm(bank[bk][:, 0:256], self.aT[:, c, tt * 128:(tt + 1) * 128], wv[:, c, :],
                                c == 0, c == NCH - 1, [wvb, self.abuf[tt // 4]], [bbuf[bk]])
                    self.cp(VA[:, tt, :, 0:64], bank[bk][:, 0:256].rearrange("p (r d) -> p r d", r=4),
                            [bbuf[bk]], [vab])
                if stop <= 2:
                    break
                for qc in range(int(os.environ.get('MOBA_QC', '4'))):
                    par = qc % 2
                    tsl = slice(qc * 512, (qc + 1) * 512)
                    for r in range(4):
                        bk = misc_bank()
                        for c in range(NCH):
                            self.mm(bank[bk][0:64, :], wq[:, c, r * 64:(r + 1) * 64], self.aT[:, c, tsl],
                                    c == 0, c == NCH - 1, [wqb, self.abuf[qc]], [bbuf[bk]])
                        self.act(QA[0:64, par, r, :], bank[bk][0:64, :], AF.Copy, [bbuf[bk]], [qab[par][r]], scale=0.125)
                    bk = misc_bank()
                    scv = bank[bk][:, 0:128].rearrange("p (a b c) -> p a b c", a=4, b=4)
                    for qi in range(4):
                        for r in range(4):
                            self.mm(scv[:, qi, r, :], QA[0:64, par, r, qi * 128:(qi + 1) * 128], kmT[:, r, :],
                                    True, True, [qab[par][r], kmTb], [bbuf[bk]])
                    for qi in range(4):
                        own = (qc * 4 + qi) // 2
                        pb = self.mobac[:, 0, own, :]
                        ng = self.mobac[:, 1, own, :]
                        self.tt(msk[:], scv[:, qi, :, :], pb.unsqueeze(1).to_broadcast([128, 4, 8]), ALU.add,
                                [bbuf[bk], self.mobacb], [mskb])
                        for r in range(4):
                            self.vmax(mx8[:, r, :], msk[:, r, :], [mskb], [mx8b])
                        self.ts(thr[:], mx8[:, :, 2], -1e29, None, ALU.max, None, [mx8b], [thrb])
                        self.tt(cmpt[:], msk[:], thr[:].unsqueeze(2).to_broadcast([128, 4, 8]), ALU.is_lt,
                                [mskb, thrb], [cmpb])
                        self.tt(bst[:, qi, :, 64:72], cmpt[:], ng.unsqueeze(1).to_broadcast([128, 4, 8]), ALU.mult,
                                [cmpb, self.mobacb], [bstb])
                    if stop <= 3:
                        break
                    for half in range(2):
                        bk = misc_bank()
                        bv = bank[bk][:].bitcast(BF16).rearrange("p (r q) -> p r q", r=2)
                        for rr in range(2):
                            r = half * 2 + rr
                            for qi in range(4):
                                self.tr(bv[0:72, rr, qi * 128:(qi + 1) * 128], bst[:, qi, r, :], self.ident[:],
                                        [bstb, self.identb], [bbuf[bk]])
                        for rr in range(2):
                            r = half * 2 + rr
                            self.cp(QA[64:72, par, r, :], bv[64:72, rr, :], [bbuf[bk]], [qab[par][r]])
                    if stop <= 4:
                        break
                    tiles = [(r, kt) for r in range(4) for kt in range(4 * qc + 4)]
                    nt = len(tiles)

                    def s_stage(i):
                        r, kt = tiles[i]
                        jd = max(0, kt - 4 * qc)
                        cols = slice(jd * 128, 512)
                        sb_ = i % 3
                        self.mm(bank[sb_][:, cols], KA[0:72, r, kt * 128:(kt + 1) * 128], QA[0:72, par, r, cols],
                                True, True, [kab[r], qab[par][r]], [bbuf[sb_]])
                        self.act(pt[:, sb_, cols], bank[sb_][:, cols], AF.Exp, [bbuf[sb_]], [ptb[sb_]])
                        if kt >= 4 * qc:
                            dcol = slice(jd * 128, (jd + 1) * 128)
                            self.tt(pt[:, sb_, dcol], pt[:, sb_, dcol], self.tri[:], ALU.mult,
                                    [ptb[sb_], self.trib], [ptb[sb_]], eng="pool")

                    def pv_stage(i):
                        r, kt = tiles[i]
                        jd = max(0, kt - 4 * qc)
                        sb_ = i % 3
                        ob = 3 + (r % 2)
                        ov = bank[ob][:, 0:260].rearrange("p (a b) -> p a b", a=4)
                        for qi in range(jd, 4):
                            self.mm(ov[:, qi, :], pt[:, sb_, qi * 128:(qi + 1) * 128], VA[:, kt, r, :],
                                    (kt == 0 and qi == 0), kt == (4 * qc + qi), [ptb[sb_], vab], [bbuf[ob]], skip=True)
                        if kt == 4 * qc + 3:
                            k2 = r % 2
                            self.recip(rc[:, k2, :], ov[:, :, 64], [bbuf[ob]], [rcb[k2]])
                            self.tt(otok[:, par, :, r * 64:(r + 1) * 64], ov[:, :, 0:64],
                                    rc[:, k2, :].unsqueeze(2).to_broadcast([128, 4, 64]), ALU.mult,
                                    [bbuf[ob], rcb[k2]], [otb[par]])

                    for i in range(nt + 1):
                        if i < nt:
                            s_stage(i)
                        if i >= 1:
                            pv_stage(i - 1)
                    if stop <= 5:
                        break
                    bk = misc_bank()
                    tv = bank[bk][:].bitcast(BF16).rearrange("p (f q) -> p f q", f=2)
                    for fc in range(2):
                        for qi in range(4):
                            self.tr(tv[:, fc, qi * 128:(qi + 1) * 128], otok[:, par, qi, fc * 128:(fc + 1) * 128],
                                    self.ident[:], [otb[par], self.identb], [bbuf[bk]])
                    self.cp(oT[:, par, :, :], tv, [bbuf[bk]], [oTb[par]])
                    for c2 in range(NCH):
                        for fc in range(2):
                            self.mm(bank[7][:], wo[:, fc, c2 * 128:(c2 + 1) * 128], oT[:, par, fc, :],
                                    fc == 0, fc == 1, [wob, oTb[par]], [bbuf[7]])
                        self.tt(self.hT[:, c2, tsl], self.hT[:, c2, tsl], bank[7][:], ALU.add,
                                [self.hbuf[c2][qc], bbuf[7]], [self.hbuf[c2][qc]])
            S.flush()

    def conv_phase(self, li):
        from contextlib import ExitStack
        S, nc = self.S, self.nc
        j = li // 3
        with ExitStack() as ph:
            sq_t = ph.enter_context(nc.sbuf_tensor(self.un("c_sq"), [128, 2, 512], BF16))
            rs_t = ph.enter_context(nc.sbuf_tensor(self.un("c_rs"), [128, 2, 512], F32))
            sqb = [Buf("sq0"), Buf("sq1")]
            rsb = [Buf("rs0"), Buf("rs1")]
            by_t = ph.enter_context(nc.sbuf_tensor(self.un("c_by"), [128, NCH, SEQ], BF16))
            z_t = ph.enter_context(nc.sbuf_tensor(self.un("c_z"), [128, 2, 2 + SEQ], F32))
            cg_t = ph.enter_context(nc.sbuf_tensor(self.un("c_cg"), [128, 2, 512], F32))
            y_t = ph.enter_context(nc.sbuf_tensor(self.un("c_y"), [128, 2, 512], F32))
            cw_t = ph.enter_context(nc.sbuf_tensor(self.un("c_cw"), [128, 3, NCH], F32))
            cwb = Buf("cw")
            byb = [[Buf(f"by{c}_{tc}") for tc in range(NTC)] for c in range(NCH)]
            zb = [[Buf(f"z{k}_{tc}") for tc in range(NTC)] for k in range(2)]
            zpad = [Buf("zp0"), Buf("zp1")]
            cgb = [Buf("cg0"), Buf("cg1")]
            yb = [Buf("y0"), Buf("y1")]
            self.dma("sp", cw_t[:].rearrange("p a c -> p (a c)"), self.d_conv_w[:, :], [], [cwb], S.dsem("cwsem"))
            for k in range(2):
                self.memset(z_t[:, k, 0:2], 0.0, [zpad[k]])
            self.norm_to_a(li, sq_t, rs_t, sqb, rsb)
            it = 0
            for c in range(NCH):
                wv = []
                for part in range(3):
                    cs = slice(part * D + c * 128, part * D + (c + 1) * 128)
                    wv.append(self.wload(lambda v: v[:, 0:1024].rearrange("p (c n) -> p c n", c=NCH),
                                         self.d_conv_w_in[j, :, cs].rearrange("(c p) n -> p c n", p=128)))
                zk = c % 2
                for tc in range(NTC):
                    tsl = slice(tc * 512, (tc + 1) * 512)
                    bs = 3 * (it % 2)
                    for part in range(3):
                        _, wb, wvw = wv[part]
                        for cc in range(NCH):
                            self.mm(self.bank[bs + part][:], wvw[:, cc, :], self.aT[:, cc, tsl], cc == 0, cc == NCH - 1,
                                    [wb, self.abuf[tc]], [self.bbuf[bs + part]])
                    k = it % 2
                    self.act(cg_t[:, k, :], self.bank[bs + 1][:], AF.Copy, [self.bbuf[bs + 1]], [cgb[k]])
                    zs = slice(2 + tc * 512, 2 + (tc + 1) * 512)
                    self.tt(z_t[:, zk, zs], cg_t[:, k, :], self.bank[bs + 2][:], ALU.mult,
                            [cgb[k], self.bbuf[bs + 2]], [zb[zk][tc]])
                    zr = [zb[zk][tc]] + ([zb[zk][tc - 1]] if tc > 0 else [zpad[zk]])
                    self.ts(y_t[:, k, :], z_t[:, zk, zs], cw_t[:, 2, c:c + 1], None, ALU.mult, None,
                            zr + [cwb], [yb[k]])
                    self.stt(y_t[:, k, :], z_t[:, zk, 1 + tc * 512:1 + (tc + 1) * 512], cw_t[:, 1, c:c + 1],
                             y_t[:, k, :], ALU.mult, ALU.add, zr + [cwb, yb[k]], [yb[k]])
                    self.stt(y_t[:, k, :], z_t[:, zk, tc * 512:(tc + 1) * 512], cw_t[:, 0, c:c + 1],
                             y_t[:, k, :], ALU.mult, ALU.add, zr + [cwb, yb[k]], [yb[k]])
                    self.tt(by_t[:, c, tsl], y_t[:, k, :], self.bank[bs][:], ALU.mult,
                            [yb[k], self.bbuf[bs]], [byb[c][tc]])
                    it += 1
            wo = []
            for q in range(4):
                wo.append(self.wload(lambda v: v.rearrange("p (c n) -> p c n", c=NCH),
                                     self.d_conv_w_out[j, :, q * 256:(q + 1) * 256].rearrange("(c p) n -> p c n", p=128)))
            self.out_proj(lambda c, tc: by_t[:, c, tc * 512:(tc + 1) * 512], lambda c, tc: byb[c][tc], wo, NCH)
            S.flush()

    def out_proj(self, src_fn, src_buf, wo, nk, banks=(6, 7)):
        n = 0
        for tc in range(NTC):
            tsl = slice(tc * 512, (tc + 1) * 512)
            for c2 in range(NCH):
                bk = banks[n % 2]
                n += 1
                _, wb, wv = wo[c2 // 2]
                for c in range(nk):
                    self.mm(self.bank[bk][:], wv[:, c, (c2 % 2) * 128:(c2 % 2 + 1) * 128], src_fn(c, tc),
                            c == 0, c == nk - 1, [wb, src_buf(c, tc)], [self.bbuf[bk]])
                self.tt(self.hT[:, c2, tsl], self.hT[:, c2, tsl], self.bank[bk][:], ALU.add,
                        [self.hbuf[c2][tc], self.bbuf[bk]], [self.hbuf[c2][tc]])


_PROG_CACHE = {}


def _get_prog(layers, final_norm):
    key = (tuple(layers), final_norm)
    if key not in _PROG_CACHE:
        p = Prog(list(layers), final_norm)
        p.build()
        _PROG_CACHE[key] = p
    return _PROG_CACHE[key]


def _f32(a):
    return np.ascontiguousarray(np.asarray(a, dtype=np.float32))


def run_layers(inputs, layers, final_norm, x_override=None, n_cores=8):
    prog = _get_prog(layers, final_norm)
    x = _f32(inputs["x"] if x_override is None else x_override)
    nm, nf, nfin = _f32(inputs["norm_mix"]), _f32(inputs["norm_ffn"]), _f32(inputs["norm_final"])
    allg = np.concatenate([nm, nf, nfin[None, :]], axis=0)
    gains = np.ascontiguousarray(allg.reshape(9, NCH, 128).transpose(2, 0, 1).reshape(128, 9 * NCH))
    cw = _f32(inputs["conv_w"])[0]
    conv_w_l = np.ascontiguousarray(cw.reshape(3, NCH, 128).transpose(2, 0, 1).reshape(128, 3 * NCH))
    shared = {"gains": gains, "conv_w_l": conv_w_l}
    for k in ["ffn_w_gate", "ffn_w_up", "ffn_w_down", "nsa_w_in", "nsa_w_out", "nsa_cmp_pos_k", "nsa_cmp_pos_v",
              "nsa_cmp_k_w1", "nsa_cmp_k_w2", "nsa_cmp_v_w1", "nsa_cmp_v_w2", "moba_w_in", "moba_w_out",
              "conv_w_in", "conv_w_out"]:
        shared[k] = _f32(inputs[k])
    w_in = shared["nsa_w_in"]
    wq = w_in[:, :, 0:1024].reshape(2, D, 4, 256).transpose(0, 2, 1, 3)
    def piece(i, g):
        return w_in[:, :, 1024 + i * 256 + g * 64:1024 + i * 256 + (g + 1) * 64]
    wkv = np.stack([np.concatenate([piece(0, g), piece(0, g), piece(1, g), piece(1, g), piece(2, g), piece(4, g),
                                    piece(3, g), piece(5, g)], axis=2) for g in range(4)], axis=1)
    shared["nsa_wq"] = np.ascontiguousarray(wq)
    shared["nsa_wkv"] = np.ascontiguousarray(wkv)
    shared["nsa_wg"] = np.ascontiguousarray(w_in[:, :, 2560:2608])
    pk, pv = shared["nsa_cmp_pos_k"], shared["nsa_cmp_pos_v"]
    def posl(p):
        return p.reshape(2, 16, 128).transpose(0, 2, 1)
    shared["nsa_posT"] = np.ascontiguousarray(np.stack([posl(pk), posl(pv)], axis=1))
    for k in ["nsa_w_in", "nsa_cmp_pos_k", "nsa_cmp_pos_v"]:
        shared.pop(k)
    shared.update(prog.host_consts())
    in_maps = []
    for b in range(n_cores):
        m = dict(shared)
        m["xT"] = np.ascontiguousarray(x[b].T)
        in_maps.append(m)
    res = run_bass_kernel_spmd(prog.nc, in_maps, core_ids=list(range(n_cores)))
    out = np.stack([np.ascontiguousarray(r["outT"].T) for r in res.results], axis=0)
    return out.astype(np.float32)


def kernel(**inputs):
    return run_layers(inputs, [0, 1, 2, 3], True)
```
